# Optimizing a Trainium2 kernel written in Bass

```python
import math
import jax
import jax.numpy as jnp
from jax import lax
import numpy as np

D_MODEL = 2048
BATCH = 4
SEQ = 4096
DEPTH = 2

CHUNK = 64
N_LEFT_CHUNKS = 8
BAND = (N_LEFT_CHUNKS + 1) * CHUNK
HEAD_DIM = 64
D_ATT = D_MODEL // 2
N_HEADS_ATT = D_ATT // HEAD_DIM
REL_CLIP = 128
N_REL = (CHUNK - 1) + REL_CLIP + 1
D_RWKV = D_MODEL // 2
N_HEADS_RWKV = D_RWKV // HEAD_DIM
DECAY_LORA = 64
AAA_LORA = 64
GATE_LORA = 128
N_B_IN = 3 * D_RWKV + DECAY_LORA + AAA_LORA + GATE_LORA
N_IN_AB = 3 * D_ATT + N_B_IN
D_SSM = D_MODEL // 2
SSM_GROUP = 16
N_SSM_GROUPS = D_SSM // SSM_GROUP
SSM_STATE = 64
D_FF = 5632
D_PLE = 256
RMS_EPS = 1e-6
GN_EPS = 64e-5

kernel_name = 'hybrid_chunk_causal_encoder'


def rms_norm(x, g):
    xf = x.astype(jnp.float32)
    y = xf * lax.rsqrt(jnp.mean(xf * xf, axis=-1, keepdims=True) + RMS_EPS)
    return (y * g.astype(jnp.float32)).astype(x.dtype)


def swiglu_ffn(x, w_gate, w_up, w_down):
    return (jax.nn.silu(x @ w_gate) * (x @ w_up)) @ w_down


def rel_bias_index():
    i = np.arange(CHUNK)[:, None]
    j = np.arange(BAND)[None, :]
    dist = i + N_LEFT_CHUNKS * CHUNK - j
    return np.clip(dist, -(CHUNK - 1), REL_CLIP) + (CHUNK - 1)


def chunked_band_attention(q, k, v, q_gain, k_gain, rel_bias):
    bsz, t, h, dh = q.shape
    n_chunks = t // CHUNK
    pad = N_LEFT_CHUNKS * CHUNK
    q = rms_norm(q, q_gain) * (dh ** -0.5)
    k = rms_norm(k, k_gain)
    k_pad = jnp.pad(k, ((0, 0), (pad, 0), (0, 0), (0, 0)))
    v_pad = jnp.pad(v, ((0, 0), (pad, 0), (0, 0), (0, 0)))
    bias = rel_bias.astype(jnp.float32)[:, rel_bias_index()]
    q_chunks = jnp.swapaxes(q.reshape(bsz, n_chunks, CHUNK, h, dh), 0, 1)
    band_pos = jnp.arange(BAND)

    def one_chunk(args):
        c, q_c = args
        start = c * CHUNK
        k_b = lax.dynamic_slice_in_dim(k_pad, start, BAND, axis=1)
        v_b = lax.dynamic_slice_in_dim(v_pad, start, BAND, axis=1)
        s = jnp.einsum('bqhd,bkhd->bhqk', q_c, k_b).astype(jnp.float32) + bias
        valid = (start + band_pos) >= pad
        s = jnp.where(valid, s, -jnp.inf)
        prob = jax.nn.softmax(s, axis=-1).astype(v.dtype)
        return jnp.einsum('bhqk,bkhd->bqhd', prob, v_b)

    out = lax.map(one_chunk, (jnp.arange(n_chunks), q_chunks))
    return jnp.swapaxes(out, 0, 1).reshape(bsz, t, h, dh)


def token_shift(z, mu):
    prev = jnp.pad(z[:, :-1], ((0, 0), (1, 0), (0, 0)))
    return z + (prev - z) * mu


def rwkv7_time_mix(z, mu, w0, w_up, a0, a_up, g_up, k_k, k_a, r_k, lnx_w, lnx_b):
    f32 = jnp.float32
    bsz, t, _ = z.shape
    z = token_shift(z.astype(f32), mu.astype(f32))
    o1, o2, o3 = D_RWKV, 2 * D_RWKV, 3 * D_RWKV
    o4 = o3 + DECAY_LORA
    o5 = o4 + AAA_LORA
    r, k, v = z[..., :o1], z[..., o1:o2], z[..., o2:o3]
    xw, xa, xg = z[..., o3:o4], z[..., o4:o5], z[..., o5:]
    w_log = -jax.nn.softplus(-(w0.astype(f32) + jnp.tanh(xw) @ w_up.astype(f32))) - 0.5
    decay = jnp.exp(-jnp.exp(w_log))
    a = jax.nn.sigmoid(a0.astype(f32) + xa @ a_up.astype(f32))
    g = jax.nn.sigmoid(xg) @ g_up.astype(f32)

    def heads(u):
        return u.reshape(bsz, t, N_HEADS_RWKV, HEAD_DIM)

    kk = heads(k * k_k.astype(f32))
    kk = kk / jnp.maximum(jnp.sqrt(jnp.sum(kk * kk, axis=-1, keepdims=True)), 1e-12)
    k = k * (1.0 + (a - 1.0) * k_a.astype(f32))
    r_h, k_h, v_h, w_h, a_h = heads(r), heads(k), heads(v), heads(decay), heads(a)

    def step(state, inp):
        r_t, w_t, k_t, v_t, ia_t, ib_t = inp
        sa = jnp.einsum('bhvk,bhk->bhv', state, ia_t)
        state = (state * w_t[:, :, None, :] + sa[..., None] * ib_t[:, :, None, :]
                 + v_t[..., None] * k_t[:, :, None, :])
        return state, jnp.einsum('bhvk,bhk->bhv', state, r_t)

    def tm(u):
        return jnp.swapaxes(u, 0, 1)

    s0 = jnp.zeros((bsz, N_HEADS_RWKV, HEAD_DIM, HEAD_DIM), f32)
    _, y = lax.scan(step, s0, (tm(r_h), tm(w_h), tm(k_h), tm(v_h), tm(-kk), tm(kk * a_h)))
    y = tm(y)
    mean = jnp.mean(y, axis=-1, keepdims=True)
    var = jnp.mean(jnp.square(y - mean), axis=-1, keepdims=True)
    y = ((y - mean) * lax.rsqrt(var + GN_EPS)).reshape(bsz, t, D_RWKV)
    y = y * lnx_w.astype(f32) + lnx_b.astype(f32)
    bonus = jnp.sum(r_h * k_h * r_k.astype(f32), axis=-1, keepdims=True) * v_h
    return (y + bonus.reshape(bsz, t, D_RWKV)) * g


def attn_rwkv_mixer(h, w_in, q_gain, k_gain, rel_bias, mu, w0, w_up, a0, a_up, g_up,
                    k_k, k_a, r_k, lnx_w, lnx_b, w_out):
    bsz, t, _ = h.shape
    proj = h @ w_in

    def heads(u):
        return u.reshape(bsz, t, N_HEADS_ATT, HEAD_DIM)

    q = heads(proj[..., :D_ATT])
    k = heads(proj[..., D_ATT:2 * D_ATT])
    v = heads(proj[..., 2 * D_ATT:3 * D_ATT])
    att = chunked_band_attention(q, k, v, q_gain, k_gain, rel_bias).reshape(bsz, t, D_ATT)
    rw = rwkv7_time_mix(proj[..., 3 * D_ATT:], mu, w0, w_up, a0, a_up, g_up,
                        k_k, k_a, r_k, lnx_w, lnx_b).astype(att.dtype)
    return jnp.concatenate([att, rw], axis=-1) @ w_out


def s5_ssm(u, lam_re, lam_im, log_dt, b_re, b_im, c_re, c_im, d_skip):
    f32 = jnp.float32
    bsz, t, _ = u.shape
    G, P, GS = N_SSM_GROUPS, SSM_STATE, SSM_GROUP
    uf = u.astype(f32).reshape(bsz, t, G, GS)
    lr, li = lam_re.astype(f32), lam_im.astype(f32)
    dt = jnp.exp(log_dt.astype(f32))[:, None]
    mag = jnp.exp(lr * dt)
    ab_re, ab_im = mag * jnp.cos(li * dt), mag * jnp.sin(li * dt)
    denom = lr * lr + li * li
    z_re = ((ab_re - 1.0) * lr + ab_im * li) / denom
    z_im = (ab_im * lr - (ab_re - 1.0) * li) / denom
    br, bi = b_re.astype(f32), b_im.astype(f32)
    bb_re = z_re[..., None] * br - z_im[..., None] * bi
    bb_im = z_re[..., None] * bi + z_im[..., None] * br
    bu_re = jnp.einsum('gpc,btgc->btgp', bb_re, uf)
    bu_im = jnp.einsum('gpc,btgc->btgp', bb_im, uf)
    a_re = jnp.broadcast_to(ab_re[None, None], (1, t, G, P))
    a_im = jnp.broadcast_to(ab_im[None, None], (1, t, G, P))

    def combine(left, right):
        al_re, al_im, bl_re, bl_im = left
        ar_re, ar_im, br_re, br_im = right
        return (ar_re * al_re - ar_im * al_im,
                ar_re * al_im + ar_im * al_re,
                ar_re * bl_re - ar_im * bl_im + br_re,
                ar_re * bl_im + ar_im * bl_re + br_im)

    _, _, h_re, h_im = lax.associative_scan(combine, (a_re, a_im, bu_re, bu_im), axis=1)
    y = (jnp.einsum('gcp,btgp->btgc', c_re.astype(f32), h_re)
         - jnp.einsum('gcp,btgp->btgc', c_im.astype(f32), h_im))
    y = y + d_skip.astype(f32).reshape(G, GS) * uf
    return y.reshape(bsz, t, D_SSM).astype(u.dtype)


def s5_mixer(h, w_in, lam_re, lam_im, log_dt, b_re, b_im, c_re, c_im, d_skip, w_out):
    y = jax.nn.gelu(s5_ssm(h @ w_in, lam_re, lam_im, log_dt, b_re, b_im, c_re, c_im, d_skip))
    z = y @ w_out
    return z[..., :D_MODEL] * jax.nn.sigmoid(z[..., D_MODEL:])


def setup_inputs(seed: int = 0) -> dict:
    key = jax.random.key(seed)
    ks = iter(jax.random.split(key, 48))
    f32 = jnp.float32
    ne, no = (DEPTH + 1) // 2, DEPTH // 2
    G, P, GS = N_SSM_GROUPS, SSM_STATE, SSM_GROUP

    def normal(shape, scale):
        return jax.random.normal(next(ks), shape, f32) * scale

    def gain(shape):
        return 1.0 + normal(shape, 0.02)

    x = normal((BATCH, SEQ, D_MODEL), 1.0)
    p = normal((DEPTH, BATCH, SEQ, D_PLE), 1.0)
    ffn1_norm = gain((DEPTH, D_MODEL))
    ffn1_w_gate = normal((DEPTH, D_MODEL, D_FF), D_MODEL ** -0.5)
    ffn1_w_up = normal((DEPTH, D_MODEL, D_FF), D_MODEL ** -0.5)
    ffn1_w_down = normal((DEPTH, D_FF, D_MODEL), D_FF ** -0.5)
    mix_norm = gain((DEPTH, D_MODEL))
    ffn2_norm = gain((DEPTH, D_MODEL))
    ffn2_w_gate = normal((DEPTH, D_MODEL, D_FF), D_MODEL ** -0.5)
    ffn2_w_up = normal((DEPTH, D_MODEL, D_FF), D_MODEL ** -0.5)
    ffn2_w_down = normal((DEPTH, D_FF, D_MODEL), D_FF ** -0.5)
    ple_norm = gain((DEPTH, D_MODEL))
    ple_w_gate = normal((DEPTH, D_MODEL, D_MODEL), D_MODEL ** -0.5)
    ple_w_proj = normal((DEPTH, D_PLE, D_MODEL), D_PLE ** -0.5)
    ab_w_in = normal((ne, D_MODEL, N_IN_AB), D_MODEL ** -0.5)
    att_q_gain = gain((ne, HEAD_DIM))
    att_k_gain = gain((ne, HEAD_DIM))
    att_rel_bias = normal((ne, N_HEADS_ATT, N_REL), 0.1)
    rwkv_mu = jax.random.uniform(next(ks), (ne, N_B_IN), f32)
    rwkv_w0 = jnp.linspace(-6.0, -1.0, D_RWKV, dtype=f32) + normal((ne, D_RWKV), 0.1)
    rwkv_w_up = normal((ne, DECAY_LORA, D_RWKV), 0.1 * DECAY_LORA ** -0.5)
    rwkv_a0 = normal((ne, D_RWKV), 0.1)
    rwkv_a_up = normal((ne, AAA_LORA, D_RWKV), 0.5 * AAA_LORA ** -0.5)
    rwkv_g_up = normal((ne, GATE_LORA, D_RWKV), GATE_LORA ** -0.5)
    rwkv_k_k = 0.85 + normal((ne, D_RWKV), 0.02)
    rwkv_k_a = gain((ne, D_RWKV))
    rwkv_r_k = normal((ne, N_HEADS_RWKV, HEAD_DIM), 0.1)
    rwkv_lnx_w = gain((ne, D_RWKV))
    rwkv_lnx_b = normal((ne, D_RWKV), 0.02)
    ab_w_out = normal((ne, D_ATT + D_RWKV, D_MODEL), (D_ATT + D_RWKV) ** -0.5)
    ssm_w_in = normal((no, D_MODEL, D_SSM), D_MODEL ** -0.5)
    ssm_lambda_re = -0.5 + normal((no, G, P), 0.01)
    ssm_lambda_im = math.pi * jnp.arange(P, dtype=f32) + normal((no, G, P), 0.01)
    ssm_log_dt = jax.random.uniform(next(ks), (no, G), f32, math.log(1e-3), math.log(1e-1))
    ssm_b_re = normal((no, G, P, GS), (2 * GS) ** -0.5)
    ssm_b_im = normal((no, G, P, GS), (2 * GS) ** -0.5)
    ssm_c_re = normal((no, G, GS, P), (2 * P) ** -0.5)
    ssm_c_im = normal((no, G, GS, P), (2 * P) ** -0.5)
    ssm_d = normal((no, D_SSM), 1.0)
    ssm_w_out = normal((no, D_SSM, 2 * D_MODEL), D_SSM ** -0.5)
    return {
        'x': x, 'p': p,
        'ffn1_norm': ffn1_norm, 'ffn1_w_gate': ffn1_w_gate, 'ffn1_w_up': ffn1_w_up,
        'ffn1_w_down': ffn1_w_down, 'mix_norm': mix_norm,
        'ffn2_norm': ffn2_norm, 'ffn2_w_gate': ffn2_w_gate, 'ffn2_w_up': ffn2_w_up,
        'ffn2_w_down': ffn2_w_down,
        'ple_norm': ple_norm, 'ple_w_gate': ple_w_gate, 'ple_w_proj': ple_w_proj,
        'ab_w_in': ab_w_in, 'att_q_gain': att_q_gain, 'att_k_gain': att_k_gain,
        'att_rel_bias': att_rel_bias, 'rwkv_mu': rwkv_mu, 'rwkv_w0': rwkv_w0,
        'rwkv_w_up': rwkv_w_up, 'rwkv_a0': rwkv_a0, 'rwkv_a_up': rwkv_a_up,
        'rwkv_g_up': rwkv_g_up, 'rwkv_k_k': rwkv_k_k, 'rwkv_k_a': rwkv_k_a,
        'rwkv_r_k': rwkv_r_k, 'rwkv_lnx_w': rwkv_lnx_w, 'rwkv_lnx_b': rwkv_lnx_b,
        'ab_w_out': ab_w_out,
        'ssm_w_in': ssm_w_in, 'ssm_lambda_re': ssm_lambda_re, 'ssm_lambda_im': ssm_lambda_im,
        'ssm_log_dt': ssm_log_dt, 'ssm_b_re': ssm_b_re, 'ssm_b_im': ssm_b_im,
        'ssm_c_re': ssm_c_re, 'ssm_c_im': ssm_c_im, 'ssm_d': ssm_d, 'ssm_w_out': ssm_w_out,
    }


def reference(x, p, ffn1_norm, ffn1_w_gate, ffn1_w_up, ffn1_w_down, mix_norm,
              ffn2_norm, ffn2_w_gate, ffn2_w_up, ffn2_w_down,
              ple_norm, ple_w_gate, ple_w_proj,
              ab_w_in, att_q_gain, att_k_gain, att_rel_bias, rwkv_mu, rwkv_w0,
              rwkv_w_up, rwkv_a0, rwkv_a_up, rwkv_g_up, rwkv_k_k, rwkv_k_a,
              rwkv_r_k, rwkv_lnx_w, rwkv_lnx_b, ab_w_out,
              ssm_w_in, ssm_lambda_re, ssm_lambda_im, ssm_log_dt, ssm_b_re, ssm_b_im,
              ssm_c_re, ssm_c_im, ssm_d, ssm_w_out):
    h = x
    for i in range(DEPTH):
        j = i // 2
        h = h + 0.5 * swiglu_ffn(rms_norm(h, ffn1_norm[i]), ffn1_w_gate[i],
                                 ffn1_w_up[i], ffn1_w_down[i])
        hn = rms_norm(h, mix_norm[i])
        if i % 2 == 0:
            mix = attn_rwkv_mixer(hn, ab_w_in[j], att_q_gain[j], att_k_gain[j],
                                  att_rel_bias[j], rwkv_mu[j], rwkv_w0[j], rwkv_w_up[j],
                                  rwkv_a0[j], rwkv_a_up[j], rwkv_g_up[j], rwkv_k_k[j],
                                  rwkv_k_a[j], rwkv_r_k[j], rwkv_lnx_w[j], rwkv_lnx_b[j],
                                  ab_w_out[j])
        else:
            mix = s5_mixer(hn, ssm_w_in[j], ssm_lambda_re[j], ssm_lambda_im[j],
                           ssm_log_dt[j], ssm_b_re[j], ssm_b_im[j], ssm_c_re[j],
                           ssm_c_im[j], ssm_d[j], ssm_w_out[j])
        h = h + mix
        h = h + 0.5 * swiglu_ffn(rms_norm(h, ffn2_norm[i]), ffn2_w_gate[i],
                                 ffn2_w_up[i], ffn2_w_down[i])
        gate = jax.nn.sigmoid(rms_norm(h, ple_norm[i]) @ ple_w_gate[i])
        h = h + gate * (p[i] @ ple_w_proj[i])
    return h
```

```python
import contextlib
import math
import numpy as np
import concourse.bass as bass
import concourse.mybir as mybir
from concourse.bass_utils import run_bass_kernel_spmd

F32 = mybir.dt.float32
F32R = mybir.dt.float32r
AF = mybir.ActivationFunctionType
ALU = mybir.AluOpType
AX = mybir.AxisListType

D = 2048
DFF = 5632
NFC = DFF // 128
FG = 11
TT = 512
RMS_EPS = 1e-6


class Prog:
    def __init__(self, nc, stack, same_engine_sync=True):
        self.nc = nc
        self.stack = stack
        self.items = {e: [] for e in ("pe", "act", "dve", "pool", "sp")}
        self.cnt = {e: 0 for e in self.items}
        self.seen = {e: {} for e in self.items}
        self.lastw = {}
        self.readers = {}
        self.dmacnt = {}
        self.semkeys = []
        self.same_engine_sync = same_engine_sync
        self._n = 0

    def _semkey(self, k):
        if k not in self.semkeys:
            self.semkeys.append(k)

    def _deps(self, e, selfkey, reads, writes):
        need = {}

        def add(p):
            if p is None:
                return
            k, v = p
            if k == selfkey and not (self.same_engine_sync and k in ("act", "dve", "pool")):
                return
            if need.get(k, 0) < v:
                need[k] = v

        for r in reads:
            add(self.lastw.get(r))
        for w in writes:
            add(self.lastw.get(w))
            for p in self.readers.get(w, ()):
                add(p)
        for k, v in need.items():
            if self.seen[e].get(k, 0) < v:
                self.items[e].append(("wait", k, v))
                self.seen[e][k] = v
                self._semkey(k)

    def _commit(self, tok, reads, writes):
        for r in reads:
            self.readers.setdefault(r, []).append(tok)
        for w in writes:
            self.lastw[w] = tok
            self.readers[w] = []

    def op(self, e, fn, reads=(), writes=()):
        self._deps(e, e, reads, writes)
        self.cnt[e] += 1
        name, a, k = fn(_REC)
        self.items[e].append(("op", (lambda eng, name=name, a=a, k=k: getattr(eng, name)(*a, **k))))
        self._semkey(e)
        self._commit((e, self.cnt[e]), reads, writes)

    def dma(self, q, out, in_, reads=(), writes=()):
        key = ("dw", writes[0]) if writes else ("dr", reads[0])
        self._deps(q, key, reads, writes)
        self.dmacnt[key] = self.dmacnt.get(key, 0) + 16
        self._semkey(key)
        self.items[q].append(("dma", (lambda eng, o=out, i=in_: eng.dma_start(out=o, in_=i)), key))
        self._commit((key, self.dmacnt[key]), reads, writes)

    def finish(self):
        for key, v in self.dmacnt.items():
            if self.seen["sp"].get(key, 0) < v:
                self.items["sp"].append(("wait", key, v))
                self.seen["sp"][key] = v

    def emit(self):
        nc = self.nc
        sems = {}
        for i, k in enumerate(self.semkeys):
            sems[k] = self.stack.enter_context(nc.semaphore("s%d" % i))
        block = self.stack.enter_context(nc.Block())

        def mk(ename):
            def body(eng):
                for it in self.items[ename]:
                    if it[0] == "wait":
                        eng.wait_ge(sems[it[1]], it[2])
                    elif it[0] == "op":
                        it[1](eng).then_inc(sems[ename], 1)
                    else:
                        it[1](eng).then_inc(sems[it[2]], 16)
            return body

        block.tensor(mk("pe"))
        block.scalar(mk("act"))
        block.vector(mk("dve"))
        block.gpsimd(mk("pool"))
        block.sync(mk("sp"))


class _Rec:
    def __getattr__(self, name):
        return lambda *a, **k: (name, a, k)


_REC = _Rec()


def r32(ap):
    return ap.bitcast(F32R)


class Dense:
    def __init__(self, nc, stack, ntok):
        self.nc = nc
        self.stack = stack
        self.P = Prog(nc, stack)
        self.ntok = ntok
        self.dram = {}
        sb = lambda name, shape, dt=F32: stack.enter_context(nc.sbuf_tensor(name, shape, dt))
        self.h = sb("h", [128, 16, TT])
        self.xn = sb("xn", [128, 16, TT], F32R)
        self.act = sb("act", [128, FG, TT], F32R)
        self.rstd = sb("rstd", [128, TT])
        self.tmp = [sb("tmp%d" % i, [128, TT]) for i in range(2)]
        self.wg = [sb("wg%d" % i, [128, 16, 128], F32R) for i in range(2)]
        self.wu = [sb("wu%d" % i, [128, 16, 128], F32R) for i in range(2)]
        self.wd = [sb("wd%d" % i, [128, FG, 128], F32R) for i in range(2)]
        self.ob = [sb("ob%d" % i, [128, TT]) for i in range(2)]
        self.pt = sb("pt", [128, 2, TT], F32R)
        self.ones = sb("ones", [128, 128])
        self.gains = {}
        self.ps = [stack.enter_context(nc.psum_tensor("ps%d" % i, [128, TT], F32)) for i in range(8)]
        self.psi = 0
        self.cnt = {}
        self.P.op("pool", lambda e: e.memset(self.ones[:], 1.0), writes=["ones"])

    def rot(self, name, n=2):
        v = self.cnt.get(name, 0)
        self.cnt[name] = v + 1
        return v % n

    def bank(self):
        i = self.psi
        self.psi = (i + 1) % 8
        return i

    def din(self, name, shape, dt=F32):
        t = self.nc.dram_tensor(name, list(shape), dt, kind="ExternalInput").ap()
        self.dram[name] = t
        return t

    def dout(self, name, shape):
        t = self.nc.dram_tensor(name, list(shape), F32, kind="ExternalOutput").ap()
        self.dram[name] = t
        return t

    def load_gain(self, name):
        g = self.stack.enter_context(self.nc.sbuf_tensor("g_" + name, [128, 16], F32))
        d = self.din(name, [128, 16])
        self.P.dma("sp", g[:], d[:, :], writes=["g_" + name])
        self.gains[name] = g
        return g

    def load_w(self, buf, res, wd_ap, kc):
        self.P.dma("sp", buf[:, 0:kc, :].rearrange("p k m -> p (k m)"), wd_ap.rearrange("p k m -> p (k m)"), writes=[res])

    def mm(self, bank, buf, wres, x, xres, kc):
        ps = self.ps[bank]
        for k in range(kc):
            self.P.op("pe", lambda e, k=k, ps=ps, buf=buf, x=x: e.matmul(
                ps[:, :], buf[:, k, :], x[:, k, :], start=(k == 0), stop=(k == kc - 1)),
                reads=[wres, xres], writes=["ps%d" % bank])

    def load_h(self, d, t0):
        self.P.dma("pool", self.h[:], d.rearrange("(c p) t -> p c t", p=128)[:, :, t0:t0 + TT], writes=["h"])

    def store_h(self, d, t0):
        self.P.dma("pool", d.rearrange("(c p) t -> p c t", p=128)[:, :, t0:t0 + TT], self.h[:], reads=["h"])

    def load_x(self, d, t0, kc):
        self.P.dma("pool", self.xn[:, 0:kc, :], d.rearrange("(c p) t -> p c t", p=128)[:, :, t0:t0 + TT], writes=["xn"])

    def norm(self, gname):
        P = self.P
        g = self.gains[gname]
        h, xn, rstd = self.h, self.xn, self.rstd
        b = self.bank()
        ps = self.ps[b]
        for c in range(16):
            ts = self.rot("tmp")
            tmp = self.tmp[ts]
            P.op("act", lambda e, c=c, tmp=tmp: e.activation(tmp[:], h[:, c, :], AF.Square), reads=["h"], writes=["tmp%d" % ts])
            P.op("pe", lambda e, c=c, tmp=tmp: e.matmul(ps[:, :], self.ones[:, :], tmp[:], start=(c == 0), stop=(c == 15)),
                 reads=["ones", "tmp%d" % ts], writes=["ps%d" % b])
        P.op("dve", lambda e: e.tensor_scalar(rstd[:], ps[:, :], 1.0 / D, RMS_EPS, ALU.mult, ALU.add),
             reads=["ps%d" % b], writes=["rstd"])
        P.op("act", lambda e: e.activation(rstd[:], rstd[:], AF.Sqrt), reads=["rstd"], writes=["rstd"])
        P.op("dve", lambda e: e.reciprocal(rstd[:], rstd[:]), reads=["rstd"], writes=["rstd"])
        for c in range(16):
            P.op("dve", lambda e, c=c: e.scalar_tensor_tensor(xn[:, c, :], h[:, c, :], g[:, c:c + 1], rstd[:],
                                                              ALU.mult, ALU.mult),
                 reads=["h", "rstd", "g_" + gname], writes=["xn"])

    def ffn(self, gname, wgd, wud, wdd):
        P = self.P
        self.norm(gname)
        for g in range(NFC // FG):
            for jj in range(FG):
                j = g * FG + jj
                s = self.rot("wgu")
                self.load_w(self.wg[s], "wg%d" % s, wgd[j], 16)
                self.load_w(self.wu[s], "wu%d" % s, wud[j], 16)
                ba, bb = self.bank(), self.bank()
                self.mm(ba, self.wg[s], "wg%d" % s, self.xn, "xn", 16)
                self.mm(bb, self.wu[s], "wu%d" % s, self.xn, "xn", 16)
                ts = self.rot("tmp")
                tmp = self.tmp[ts]
                P.op("act", lambda e, tmp=tmp, ba=ba: e.activation(tmp[:], self.ps[ba][:, :], AF.Silu),
                     reads=["ps%d" % ba], writes=["tmp%d" % ts])
                P.op("dve", lambda e, tmp=tmp, bb=bb, jj=jj: e.tensor_tensor(self.act[:, jj, :], tmp[:], self.ps[bb][:, :], ALU.mult),
                     reads=["tmp%d" % ts, "ps%d" % bb], writes=["act"])
            for dc in range(16):
                s = self.rot("wd")
                self.load_w(self.wd[s], "wd%d" % s, wdd[dc, :, g * FG:(g + 1) * FG, :], FG)
                b = self.bank()
                self.mm(b, self.wd[s], "wd%d" % s, self.act, "act", FG)
                P.op("dve", lambda e, b=b, dc=dc: e.scalar_tensor_tensor(self.h[:, dc, :], self.ps[b][:, :], 0.5, self.h[:, dc, :],
                                                                         ALU.mult, ALU.add),
                     reads=["ps%d" % b, "h"], writes=["h"])

    def linear_out(self, wdram, nch, kc, out_d, t0):
        P = self.P
        for j in range(nch):
            s = self.rot("wgu")
            self.load_w(self.wg[s], "wg%d" % s, wdram[j], kc)
            b = self.bank()
            self.mm(b, self.wg[s], "wg%d" % s, self.xn, "xn", kc)
            os_ = self.rot("ob")
            ob = self.ob[os_]
            P.op("act", lambda e, ob=ob, b=b: e.activation(ob[:], self.ps[b][:, :], AF.Copy),
                 reads=["ps%d" % b], writes=["ob%d" % os_])
            P.dma("pool", out_d[j * 128:(j + 1) * 128, t0:t0 + TT], ob[:], reads=["ob%d" % os_])

    def linear_res(self, wdram, kc):
        P = self.P
        for dc in range(16):
            s = self.rot("wgu")
            self.load_w(self.wg[s], "wg%d" % s, wdram[dc], kc)
            b = self.bank()
            self.mm(b, self.wg[s], "wg%d" % s, self.xn, "xn", kc)
            P.op("dve", lambda e, b=b, dc=dc: e.tensor_tensor(self.h[:, dc, :], self.ps[b][:, :], self.h[:, dc, :], ALU.add),
                 reads=["ps%d" % b, "h"], writes=["h"])

    def glu_res(self, wdram, kc):
        P = self.P
        for dc in range(16):
            s = self.rot("wgu")
            self.load_w(self.wg[s], "wg%d" % s, wdram[dc], kc)
            self.load_w(self.wu[s], "wu%d" % s, wdram[16 + dc], kc)
            ba, bb = self.bank(), self.bank()
            self.mm(ba, self.wg[s], "wg%d" % s, self.xn, "xn", kc)
            self.mm(bb, self.wu[s], "wu%d" % s, self.xn, "xn", kc)
            ts = self.rot("tmp")
            tmp = self.tmp[ts]
            P.op("act", lambda e, tmp=tmp, bb=bb: e.activation(tmp[:], self.ps[bb][:, :], AF.Sigmoid),
                 reads=["ps%d" % bb], writes=["tmp%d" % ts])
            P.op("dve", lambda e, tmp=tmp, ba=ba: e.tensor_tensor(tmp[:], tmp[:], self.ps[ba][:, :], ALU.mult),
                 reads=["tmp%d" % ts, "ps%d" % ba], writes=["tmp%d" % ts])
            P.op("dve", lambda e, tmp=tmp, dc=dc: e.tensor_tensor(self.h[:, dc, :], tmp[:], self.h[:, dc, :], ALU.add),
                 reads=["tmp%d" % ts, "h"], writes=["h"])

    def ple(self, gname, wgate, wproj, p_d, t0):
        P = self.P
        self.norm(gname)
        P.dma("pool", self.pt[:], p_d.rearrange("(c p) t -> p c t", p=128)[:, :, t0:t0 + TT], writes=["pt"])
        for dc in range(16):
            s = self.rot("wgu")
            self.load_w(self.wg[s], "wg%d" % s, wgate[dc], 16)
            self.load_w(self.wu[s], "wu%d" % s, wproj[dc], 2)
            ba, bb = self.bank(), self.bank()
            self.mm(ba, self.wg[s], "wg%d" % s, self.xn, "xn", 16)
            self.mm(bb, self.wu[s], "wu%d" % s, self.pt, "pt", 2)
            ts = self.rot("tmp")
            tmp = self.tmp[ts]
            P.op("act", lambda e, tmp=tmp, ba=ba: e.activation(tmp[:], self.ps[ba][:, :], AF.Sigmoid),
                 reads=["ps%d" % ba], writes=["tmp%d" % ts])
            P.op("dve", lambda e, tmp=tmp, bb=bb: e.tensor_tensor(tmp[:], tmp[:], self.ps[bb][:, :], ALU.mult),
                 reads=["tmp%d" % ts, "ps%d" % bb], writes=["tmp%d" % ts])
            P.op("dve", lambda e, tmp=tmp, dc=dc: e.tensor_tensor(self.h[:, dc, :], tmp[:], self.h[:, dc, :], ALU.add),
                 reads=["tmp%d" % ts, "h"], writes=["h"])


def wlayout(w):
    K, N = w.shape
    return np.ascontiguousarray(w.reshape(K // 128, 128, N // 128, 128).transpose(2, 1, 0, 3))


def glayout(g):
    return np.ascontiguousarray(g.reshape(16, 128).T)


def fm(a):
    return np.ascontiguousarray(a.T)


def build_dense(which, ntok):
    nc = bass.Bass("TRN2", target_bir_lowering=False)
    nc.dge_precook = False
    stack = contextlib.ExitStack()
    dn = Dense(nc, stack, ntok)
    hin = dn.din("h_in", [D, ntok])
    W = {}

    def ffn_w(pfx):
        W[pfx + "_g"] = dn.din(pfx + "_g", [NFC, 128, 16, 128], F32R)
        W[pfx + "_u"] = dn.din(pfx + "_u", [NFC, 128, 16, 128], F32R)
        W[pfx + "_d"] = dn.din(pfx + "_d", [16, 128, NFC, 128], F32R)
        dn.load_gain(pfx + "_n")

    if which == "A":
        ffn_w("f1")
        dn.load_gain("mix_n")
        W["win"] = dn.din("win", [50, 128, 16, 128], F32R)
        h_out = dn.dout("h_out", [D, ntok])
        proj = dn.dout("proj", [6400, ntok])
    elif which == "C":
        mix = dn.din("mix", [D, ntok], F32R)
        W["wout"] = dn.din("wout", [16, 128, 16, 128], F32R)
        ffn_w("f2")
        dn.load_gain("ple_n")
        W["pg"] = dn.din("pg", [16, 128, 16, 128], F32R)
        W["pp"] = dn.din("pp", [16, 128, 2, 128], F32R)
        p_d = dn.din("p", [256, ntok], F32R)
        ffn_w("f1")
        dn.load_gain("mix_n")
        W["win"] = dn.din("win", [8, 128, 16, 128], F32R)
        h_out = dn.dout("h_out", [D, ntok])
        proj = dn.dout("proj", [1024, ntok])
    else:
        mix = dn.din("mix", [1024, ntok], F32R)
        W["wout"] = dn.din("wout", [32, 128, 8, 128], F32R)
        ffn_w("f2")
        dn.load_gain("ple_n")
        W["pg"] = dn.din("pg", [16, 128, 16, 128], F32R)
        W["pp"] = dn.din("pp", [16, 128, 2, 128], F32R)
        p_d = dn.din("p", [256, ntok], F32R)
        h_out = dn.dout("h_out", [D, ntok])

    for t0 in range(0, ntok, TT):
        dn.load_h(hin, t0)
        if which == "A":
            dn.ffn("f1_n", W["f1_g"], W["f1_u"], W["f1_d"])
            dn.store_h(h_out, t0)
            dn.norm("mix_n")
            dn.linear_out(W["win"], 50, 16, proj, t0)
        elif which == "C":
            dn.load_x(mix, t0, 16)
            dn.linear_res(W["wout"], 16)
            dn.ffn("f2_n", W["f2_g"], W["f2_u"], W["f2_d"])
            dn.ple("ple_n", W["pg"], W["pp"], p_d, t0)
            dn.ffn("f1_n", W["f1_g"], W["f1_u"], W["f1_d"])
            dn.store_h(h_out, t0)
            dn.norm("mix_n")
            dn.linear_out(W["win"], 8, 16, proj, t0)
        else:
            dn.load_x(mix, t0, 8)
            dn.glu_res(W["wout"], 8)
            dn.ffn("f2_n", W["f2_g"], W["f2_u"], W["f2_d"])
            dn.ple("ple_n", W["pg"], W["pp"], p_d, t0)
            dn.store_h(h_out, t0)
    dn.P.finish()
    dn.P.emit()
    stack.close()
    return nc


TWO_PI = 2.0 * math.pi


def build_s5(T, debug=False):
    nc = bass.Bass("TRN2", target_bir_lowering=False)
    nc.dge_precook = False
    stack = contextlib.ExitStack()
    P = Prog(nc, stack)
    NLEV = int(math.log2(T))
    assert (1 << NLEV) == T
    NTT = T // TT

    def din(name, shape, dt=F32):
        return nc.dram_tensor(name, list(shape), dt, kind="ExternalInput").ap()

    def sb(name, shape, dt=F32):
        return stack.enter_context(nc.sbuf_tensor(name, list(shape), dt))

    u_d = din("u_in", [512, T])
    lamre128_d, lamim128_d, logdt128_d = din("lamre128", [128, 16]), din("lamim128", [128, 16]), din("logdt128", [128, 16])
    lamre16_d, lamim16_d, logdt16_d = din("lamre16", [16, 2048]), din("lamim16", [16, 2048]), din("logdt16", [16, 2048])
    bre16_d, bim16_d = din("bre16", [16, 2048]), din("bim16", [16, 2048])
    cre_d, cim_d = din("cre_bd", [128, 16, 32]), din("cim_bd", [128, 16, 32])
    d32_d = din("d32", [32, 16])
    y_d = nc.dram_tensor("y_out", [512, T], F32, kind="ExternalOutput").ap()

    I32 = mybir.dt.int32

    def discretize(pfx, npart, nfree, lre_d, lim_d, ldt_d, scr):
        sh = [npart, nfree]
        if not scr:
            for n_ in ("lre", "lim", "dt", "mag", "ang", "q", "m", "abre", "abim"):
                scr[n_] = sb(pfx + n_, sh)
            scr["qi"] = sb(pfx + "qi", sh, I32)
        lre, lim, dtt, mag, ang, q, qi, m, abre, abim = [scr[n_] for n_ in ("lre", "lim", "dt", "mag", "ang", "q", "qi", "m", "abre", "abim")]
        R = lambda *names: [pfx + n for n in names]
        P.dma("sp", lre[:], lre_d, writes=R("lre"))
        P.dma("sp", lim[:], lim_d, writes=R("lim"))
        P.dma("sp", dtt[:], ldt_d, writes=R("dt"))
        P.op("act", lambda e: e.activation(dtt[:], dtt[:], AF.Exp), reads=R("dt"), writes=R("dt"))
        P.op("dve", lambda e: e.tensor_tensor(mag[:], lre[:], dtt[:], ALU.mult), reads=R("lre", "dt"), writes=R("mag"))
        P.op("act", lambda e: e.activation(mag[:], mag[:], AF.Exp), reads=R("mag"), writes=R("mag"))
        P.op("dve", lambda e: e.tensor_tensor(ang[:], lim[:], dtt[:], ALU.mult), reads=R("lim", "dt"), writes=R("ang"))

        def sin_of(dst, dres, shift):
            P.op("dve", lambda e: e.tensor_scalar(q[:], ang[:], shift, 1.0 / TWO_PI, ALU.add, ALU.mult), reads=R("ang"), writes=R("q"))
            P.op("dve", lambda e: e.tensor_copy(qi[:], q[:]), reads=R("q"), writes=R("qi"))
            P.op("dve", lambda e: e.tensor_copy(q[:], qi[:]), reads=R("qi"), writes=R("q"))
            P.op("dve", lambda e: e.scalar_tensor_tensor(m[:], q[:], -TWO_PI, ang[:], ALU.mult, ALU.add), reads=R("q", "ang"), writes=R("m"))
            P.op("dve", lambda e: e.tensor_scalar(m[:], m[:], shift, None, ALU.add), reads=R("m"), writes=R("m"))
            P.op("dve", lambda e: e.tensor_scalar(q[:], m[:], math.pi, -TWO_PI, ALU.is_gt, ALU.mult), reads=R("m"), writes=R("q"))
            P.op("dve", lambda e: e.tensor_tensor(m[:], m[:], q[:], ALU.add), reads=R("m", "q"), writes=R("m"))
            P.op("dve", lambda e: e.tensor_scalar(q[:], m[:], -math.pi, TWO_PI, ALU.is_lt, ALU.mult), reads=R("m"), writes=R("q"))
            P.op("dve", lambda e: e.tensor_tensor(m[:], m[:], q[:], ALU.add), reads=R("m", "q"), writes=R("m"))
            P.op("dve", lambda e: e.tensor_scalar(m[:], m[:], math.pi, -math.pi, ALU.min, ALU.max), reads=R("m"), writes=R("m"))
            P.op("act", lambda e: e.activation(dst[:], m[:], AF.Sin), reads=R("m"), writes=[dres])
            P.op("dve", lambda e: e.tensor_tensor(dst[:], dst[:], mag[:], ALU.mult), reads=[dres] + R("mag"), writes=[dres])

        sin_of(abim, pfx + "abim", 0.0)
        sin_of(abre, pfx + "abre", 0.5 * math.pi)
        return abre, abim, lre, lim

    scrA = {}
    abreA, abimA, _, _ = discretize("A", 128, 16, lamre128_d[:, :], lamim128_d[:, :], logdt128_d[:, :], scrA)
    pwre = sb("pwre", [128, NLEV, 16])
    pwim = sb("pwim", [128, NLEV, 16])
    npwim = sb("npwim", [128, NLEV, 16])
    tA = sb("tA", [128, 16])
    tB = sb("tB", [128, 16])
    P.op("dve", lambda e: e.tensor_copy(pwre[:, 0, :], abreA[:]), reads=["Aabre"], writes=["pw"])
    P.op("dve", lambda e: e.tensor_copy(pwim[:, 0, :], abimA[:]), reads=["Aabim"], writes=["pw"])
    for j in range(1, NLEV):
        P.op("dve", lambda e, j=j: e.tensor_tensor(tA[:], pwre[:, j - 1, :], pwre[:, j - 1, :], ALU.mult), reads=["pw"], writes=["tA"])
        P.op("dve", lambda e, j=j: e.tensor_tensor(tB[:], pwim[:, j - 1, :], pwim[:, j - 1, :], ALU.mult), reads=["pw"], writes=["tB"])
        P.op("dve", lambda e, j=j: e.tensor_tensor(tA[:], tA[:], tB[:], ALU.subtract), reads=["tA", "tB"], writes=["tA"])
        P.op("dve", lambda e, j=j: e.tensor_tensor(tB[:], pwre[:, j - 1, :], pwim[:, j - 1, :], ALU.mult), reads=["pw"], writes=["tB"])
        P.op("dve", lambda e, j=j: e.tensor_copy(pwre[:, j, :], tA[:]), reads=["tA"], writes=["pw"])
        P.op("dve", lambda e, j=j: e.tensor_scalar(pwim[:, j, :], tB[:], 2.0, None, ALU.mult), reads=["tB"], writes=["pw"])
    P.op("dve", lambda e: e.tensor_scalar(npwim[:], pwim[:], -1.0, None, ALU.mult), reads=["pw"], writes=["npw"])

    HB = 1024
    sh = [16, HB]
    br, bi = sb("br", sh), sb("bi", sh)
    padre = sb("padre", [16, 32, 128])
    padim = sb("padim", [16, 32, 128])
    P.op("pool", lambda e: e.memset(padre[:], 0.0), writes=["padre"])
    P.op("pool", lambda e: e.memset(padim[:], 0.0), writes=["padim"])
    scrB = {}
    for hf in range(2):
        cs = slice(hf * HB, (hf + 1) * HB)
        abreB, abimB, lreB, limB = discretize("B", 16, HB, lamre16_d[:, cs], lamim16_d[:, cs], logdt16_d[:, cs], scrB)
        den, zre, zim, t1, t2 = scrB["dt"], scrB["mag"], scrB["ang"], scrB["q"], scrB["m"]
        P.dma("sp", br[:], bre16_d[:, cs], writes=["br"])
        P.dma("sp", bi[:], bim16_d[:, cs], writes=["bi"])
        P.op("dve", lambda e: e.tensor_tensor(den[:], lreB[:], lreB[:], ALU.mult), reads=["Blre"], writes=["Bdt"])
        P.op("dve", lambda e: e.tensor_tensor(t1[:], limB[:], limB[:], ALU.mult), reads=["Blim"], writes=["Bq"])
        P.op("dve", lambda e: e.tensor_tensor(den[:], den[:], t1[:], ALU.add), reads=["Bdt", "Bq"], writes=["Bdt"])
        P.op("dve", lambda e: e.reciprocal(den[:], den[:]), reads=["Bdt"], writes=["Bdt"])
        P.op("dve", lambda e: e.tensor_scalar(abreB[:], abreB[:], -1.0, None, ALU.add), reads=["Babre"], writes=["Babre"])
        P.op("dve", lambda e: e.tensor_tensor(t1[:], abreB[:], lreB[:], ALU.mult), reads=["Babre", "Blre"], writes=["Bq"])
        P.op("dve", lambda e: e.tensor_tensor(t2[:], abimB[:], limB[:], ALU.mult), reads=["Babim", "Blim"], writes=["Bm"])
        P.op("dve", lambda e: e.tensor_tensor(t1[:], t1[:], t2[:], ALU.add), reads=["Bq", "Bm"], writes=["Bq"])
        P.op("dve", lambda e: e.tensor_tensor(zre[:], t1[:], den[:], ALU.mult), reads=["Bq", "Bdt"], writes=["Bmag"])
        P.op("dve", lambda e: e.tensor_tensor(t1[:], abimB[:], lreB[:], ALU.mult), reads=["Babim", "Blre"], writes=["Bq"])
        P.op("dve", lambda e: e.tensor_tensor(t2[:], abreB[:], limB[:], ALU.mult), reads=["Babre", "Blim"], writes=["Bm"])
        P.op("dve", lambda e: e.tensor_tensor(t1[:], t1[:], t2[:], ALU.subtract), reads=["Bq", "Bm"], writes=["Bq"])
        P.op("dve", lambda e: e.tensor_tensor(zim[:], t1[:], den[:], ALU.mult), reads=["Bq", "Bdt"], writes=["Bang"])
        v4 = lambda t: t[:].rearrange("c (pr e p) -> c pr e p", e=2, p=64)
        pv4 = lambda t, hf=hf: t[:, hf * 16:(hf + 1) * 16, :].rearrange("c (pr e) m -> c pr e m", e=2)
        P.op("dve", lambda e: e.tensor_tensor(t1[:], zre[:], br[:], ALU.mult), reads=["Bmag", "br"], writes=["Bq"])
        P.op("dve", lambda e: e.tensor_tensor(t2[:], zim[:], bi[:], ALU.mult), reads=["Bang", "bi"], writes=["Bm"])
        P.op("dve", lambda e: e.tensor_tensor(t1[:], t1[:], t2[:], ALU.subtract), reads=["Bq", "Bm"], writes=["Bq"])
        for e_ in range(2):
            P.op("dve", lambda e, e_=e_, pv4=pv4: e.tensor_copy(pv4(padre)[:, :, e_, e_ * 64:(e_ + 1) * 64], v4(t1)[:, :, e_, :]),
                 reads=["Bq"], writes=["padre"])
        P.op("dve", lambda e: e.tensor_tensor(t1[:], zre[:], bi[:], ALU.mult), reads=["Bmag", "bi"], writes=["Bq"])
        P.op("dve", lambda e: e.tensor_tensor(t2[:], zim[:], br[:], ALU.mult), reads=["Bang", "br"], writes=["Bm"])
        P.op("dve", lambda e: e.tensor_tensor(t1[:], t1[:], t2[:], ALU.add), reads=["Bq", "Bm"], writes=["Bq"])
        for e_ in range(2):
            P.op("dve", lambda e, e_=e_, pv4=pv4: e.tensor_copy(pv4(padim)[:, :, e_, e_ * 64:(e_ + 1) * 64], v4(t1)[:, :, e_, :]),
                 reads=["Bq"], writes=["padim"])

    cre, cim, d32 = sb("cre", [128, 16, 32]), sb("cim", [128, 16, 32]), sb("d32s", [32, 16])
    P.dma("sp", cre[:], cre_d[:, :, :], writes=["cre"])
    P.dma("sp", cim[:], cim_d[:, :, :], writes=["cim"])
    P.dma("sp", d32[:], d32_d[:, :], writes=["d32"])
    P.op("dve", lambda e: e.tensor_scalar(cim[:], cim[:], -1.0, None, ALU.mult), reads=["cim"], writes=["cim"])

    hA = sb("hA", [128, 2, T])
    hB = sb("hB", [128, 2, T])
    u32 = [sb("u32_%d" % i, [32, TT]) for i in range(2)]
    u16 = [[sb("u16_%d_%d" % (i, e_), [16, TT]) for e_ in range(2)] for i in range(2)]
    yb = [sb("yb%d" % i, [32, TT]) for i in range(2)]
    gt = [sb("gt%d" % i, [32, TT]) for i in range(2)]
    ps = [stack.enter_context(nc.psum_tensor("ps%d" % i, [128, TT], F32)) for i in range(8)]
    psi = [0]

    def bank():
        i = psi[0]
        psi[0] = (i + 1) % 8
        return i

    it = 0
    for pr in range(16):
        for tt in range(NTT):
            ts_ = slice(tt * TT, (tt + 1) * TT)
            s = it % 2
            it += 1
            for e_ in range(2):
                g = 2 * pr + e_
                P.dma("pool", u16[s][e_][:], u_d[g * 16:(g + 1) * 16, ts_], writes=["u16_%d_%d" % (s, e_)])
            for ri, pad in enumerate((padre, padim)):
                b = bank()
                for e_ in range(2):
                    g = 2 * pr + e_
                    P.op("pe", lambda e, b=b, pad=pad, g=g, e_=e_, s=s: e.matmul(
                        ps[b][:, :], pad[:, g, :], u16[s][e_][:, :], start=(e_ == 0), stop=(e_ == 1)),
                        reads=["padre", "padim", "u16_%d_%d" % (s, e_)], writes=["ps%d" % b])
                P.op("act", lambda e, b=b, ri=ri, ts_=ts_: e.activation(hA[:, ri, ts_], ps[b][:, :], AF.Copy),
                     reads=["ps%d" % b], writes=["hA"])
        src, dst, sres, dres = hA, hB, "hA", "hB"
        for j in range(NLEV):
            dd = 1 << j
            ar, ai, nai = pwre[:, j, pr:pr + 1], pwim[:, j, pr:pr + 1], npwim[:, j, pr:pr + 1]
            n = T - dd
            P.op("act", lambda e, src=src, dst=dst, dd=dd: e.activation(dst[:, :, 0:dd], src[:, :, 0:dd], AF.Copy),
                 reads=[sres], writes=[dres])
            P.op("dve", lambda e, src=src, dst=dst, dd=dd, n=n, ar=ar: e.scalar_tensor_tensor(
                dst[:, 0, dd:T], src[:, 0, 0:n], ar, src[:, 0, dd:T], ALU.mult, ALU.add), reads=[sres, "pw"], writes=[dres])
            P.op("dve", lambda e, src=src, dst=dst, dd=dd, n=n, nai=nai: e.scalar_tensor_tensor(
                dst[:, 0, dd:T], src[:, 1, 0:n], nai, dst[:, 0, dd:T], ALU.mult, ALU.add), reads=[sres, "npw", dres], writes=[dres])
            P.op("dve", lambda e, src=src, dst=dst, dd=dd, n=n, ar=ar: e.scalar_tensor_tensor(
                dst[:, 1, dd:T], src[:, 1, 0:n], ar, src[:, 1, dd:T], ALU.mult, ALU.add), reads=[sres, "pw"], writes=[dres])
            P.op("dve", lambda e, src=src, dst=dst, dd=dd, n=n, ai=ai: e.scalar_tensor_tensor(
                dst[:, 1, dd:T], src[:, 0, 0:n], ai, dst[:, 1, dd:T], ALU.mult, ALU.add), reads=[sres, "pw", dres], writes=[dres])
            src, dst, sres, dres = dst, src, dres, sres
        for tt in range(NTT):
            ts_ = slice(tt * TT, (tt + 1) * TT)
            b = bank()
            P.op("pe", lambda e, b=b, src=src, ts_=ts_, pr=pr: e.matmul(ps[b][0:32, :], cre[:, pr, :], src[:, 0, ts_], start=True, stop=False),
                 reads=["cre", sres], writes=["ps%d" % b])
            P.op("pe", lambda e, b=b, src=src, ts_=ts_, pr=pr: e.matmul(ps[b][0:32, :], cim[:, pr, :], src[:, 1, ts_], start=False, stop=True),
                 reads=["cim", sres], writes=["ps%d" % b])
            k = (pr * NTT + tt) % 2
            y, g_ = yb[k], gt[k]
            P.dma("pool", u32[k][:], u_d[pr * 32:(pr + 1) * 32, ts_], writes=["u32_%d" % k])
            P.op("dve", lambda e, b=b, y=y, k=k, pr=pr: e.scalar_tensor_tensor(
                y[:], u32[k][:, :], d32[:, pr:pr + 1], ps[b][0:32, :], ALU.mult, ALU.add),
                reads=["u32_%d" % k, "d32", "ps%d" % b], writes=["yb%d" % k])
            P.op("dve", lambda e, y=y, g_=g_: e.tensor_tensor(g_[:], y[:], y[:], ALU.mult), reads=["yb%d" % k], writes=["gt%d" % k])
            P.op("dve", lambda e, g_=g_: e.tensor_scalar(g_[:], g_[:], 0.044715, 1.0, ALU.mult, ALU.add), reads=["gt%d" % k], writes=["gt%d" % k])
            P.op("dve", lambda e, y=y, g_=g_: e.tensor_tensor(g_[:], g_[:], y[:], ALU.mult), reads=["gt%d" % k, "yb%d" % k], writes=["gt%d" % k])
            P.op("act", lambda e, g_=g_: e.activation(g_[:], g_[:], AF.Sigmoid, scale=2.0 * math.sqrt(2.0 / math.pi)),
                 reads=["gt%d" % k], writes=["gt%d" % k])
            P.op("dve", lambda e, y=y, g_=g_: e.tensor_tensor(y[:], y[:], g_[:], ALU.mult), reads=["gt%d" % k, "yb%d" % k], writes=["yb%d" % k])
            P.dma("sp", y_d[pr * 32:(pr + 1) * 32, ts_], y[:], reads=["yb%d" % k])
    if debug:
        for nm, t, res in (("dbg_abreA", abreA, "Aabre"), ("dbg_abimA", abimA, "Aabim"), ("dbg_pwre", pwre, "pw"), ("dbg_pwim", pwim, "pw"),
                           ("dbg_padre", padre, "padre"), ("dbg_angA", scrA["ang"], "Aang"), ("dbg_magA", scrA["mag"], "Amag"), ("dbg_dtA", scrA["dt"], "Adt"), ("dbg_mA", scrA["m"], "Am"), ("dbg_qA", scrA["q"], "Aq"), ("dbg_lreA", scrA["lre"], "Alre"), ("dbg_padim", padim, "padim"), ("dbg_hA", hA, "hA"), ("dbg_hB", hB, "hB")):
            shp = list(t[:].shape)
            dd_ = nc.dram_tensor(nm, shp, F32, kind="ExternalOutput").ap()
            P.dma("sp", dd_, t[:], reads=[res])
    P.finish()
    P.emit()
    stack.close()
    return nc


def s5_host_inputs(u, lam_re, lam_im, log_dt, b_re, b_im, c_re, c_im, d, gh):
    G0 = gh * 32
    gs = slice(G0, G0 + 32)
    lr, li, ld = lam_re[gs], lam_im[gs], log_dt[gs]
    def l128(a):
        return np.ascontiguousarray(a.reshape(16, 2, 64).transpose(1, 2, 0).reshape(128, 16))
    def l16(a):
        return np.ascontiguousarray(np.broadcast_to(a.reshape(1, 2048), (16, 2048)))
    ldb = np.broadcast_to(ld[:, None], (32, 64))
    cre = np.zeros((128, 16, 32), np.float32)
    cim = np.zeros((128, 16, 32), np.float32)
    crs, cis = c_re[gs].reshape(16, 2, 16, 64), c_im[gs].reshape(16, 2, 16, 64)
    for e_ in range(2):
        cre[e_ * 64:(e_ + 1) * 64, :, e_ * 16:(e_ + 1) * 16] = crs[:, e_].transpose(2, 0, 1)
        cim[e_ * 64:(e_ + 1) * 64, :, e_ * 16:(e_ + 1) * 16] = cis[:, e_].transpose(2, 0, 1)
    return {
        "u_in": np.ascontiguousarray(u),
        "lamre128": l128(lr), "lamim128": l128(li), "logdt128": l128(ldb),
        "lamre16": l16(lr), "lamim16": l16(li), "logdt16": l16(ldb),
        "bre16": np.ascontiguousarray(b_re[gs].transpose(2, 0, 1).reshape(16, 2048)),
        "bim16": np.ascontiguousarray(b_im[gs].transpose(2, 0, 1).reshape(16, 2048)),
        "cre_bd": cre, "cim_bd": cim,
        "d32": np.ascontiguousarray(d[G0 * 16:(G0 + 32) * 16].reshape(16, 32).T),
    }


NEG = -1.0e30


class Mixer:
    def __init__(self, T):
        self.T = T
        nc = bass.Bass("TRN2", target_bir_lowering=False)
        nc.dge_precook = False
        self.nc = nc
        self.stack = contextlib.ExitStack()
        self.P = Prog(nc, self.stack)
        self.ps = [self.stack.enter_context(nc.psum_tensor("ps%d" % i, [128, 512], F32)) for i in range(8)]
        self.psi = 0
        self.cnt = {}

    def sb(self, name, shape, dt=F32):
        return self.stack.enter_context(self.nc.sbuf_tensor(name, list(shape), dt))

    def din(self, name, shape, dt=F32):
        return self.nc.dram_tensor(name, list(shape), dt, kind="ExternalInput").ap()

    def dout(self, name, shape):
        return self.nc.dram_tensor(name, list(shape), F32, kind="ExternalOutput").ap()

    def bank(self, lo=0, hi=8):
        key = ("bank", lo, hi)
        v = self.cnt.get(key, 0)
        self.cnt[key] = v + 1
        return lo + v % (hi - lo)

    def rot(self, name, n=2):
        v = self.cnt.get(name, 0)
        self.cnt[name] = v + 1
        return v % n

    def finish(self):
        self.P.finish()
        self.P.emit()
        self.stack.close()
        return self.nc

    def attention(self):
        T, P, ps = self.T, self.P, self.ps
        NQB = T // 128
        q_d = self.din("att_q", [64, 8, T])
        k_d = self.din("att_k", [64, 8, T])
        v_d = self.din("att_v", [T, 8, 64], F32R)
        bias_d = self.din("att_bias", [128, 8, 5, 128])
        qg_d = self.din("att_qg", [64, 1])
        kg_d = self.din("att_kg", [64, 1])
        out_d = self.dout("att_out", [64, 8, T])
        bias = self.sb("abias", [128, 8, 5, 128])
        qg, kg = self.sb("aqg", [64, 1]), self.sb("akg", [64, 1])
        ones = self.sb("aones", [128, 64])
        onesr = self.sb("aonesr", [128, 64], F32R)
        P.dma("sp", bias[:], bias_d[:, :, :, :], writes=["abias"])
        P.dma("sp", qg[:], qg_d[:, :], writes=["aqg"])
        P.dma("sp", kg[:], kg_d[:, :], writes=["akg"])
        P.op("pool", lambda e: e.memset(ones[:], 1.0), writes=["aones"])
        P.op("dve", lambda e: e.tensor_copy(onesr[:], ones[:]), reads=["aones"], writes=["aonesr"])
        P.op("dve", lambda e: e.tensor_scalar(qg[:], qg[:], 0.125, None, ALU.mult), reads=["aqg"], writes=["aqg"])
        qraw = [self.sb("aqraw0", [64, T])] * 2
        kraw = [self.sb("akraw0", [64, T])] * 2
        qn = [self.sb("aqn%d" % i, [64, T], F32R) for i in range(2)]
        kn = [self.sb("akn%d" % i, [64, T], F32R) for i in range(2)]
        vt = [self.sb("avt%d" % i, [128, NQB, 64], F32R) for i in range(2)]
        ob = [self.sb("aob%d" % i, [64, T]) for i in range(2)]
        sq = [self.sb("asq%d" % i, [64, 512]) for i in range(2)]
        rs = [self.sb("ars%d" % i, [64, 512]) for i in range(2)]
        pt = [self.sb("apt%d" % i, [128, 5, 128], F32R) for i in range(2)]
        rec = [self.sb("arec%d" % i, [64, 512]) for i in range(2)]
        for h in range(8):
            s = h % 2
            P.dma("pool", qraw[s][:], q_d[:, h, :], writes=["aqraw0"])
            P.dma("pool", kraw[s][:], k_d[:, h, :], writes=["akraw0"])
            P.dma("pool", vt[s][:], v_d.rearrange("(kt p) h d -> p kt h d", p=128)[:, :, h, :], writes=["avt%d" % s])
            for raw, rawres, dst, dres, g, gres in ((qraw[s], "aqraw0", qn[s], "aqn%d" % s, qg, "aqg"),
                                                    (kraw[s], "akraw0", kn[s], "akn%d" % s, kg, "akg")):
                for tt in range(T // 512):
                    ts_ = slice(tt * 512, (tt + 1) * 512)
                    i = self.rot("asq")
                    b = self.bank(4, 8)
                    P.op("act", lambda e, i=i, raw=raw, ts_=ts_: e.activation(sq[i][:], raw[:, ts_], AF.Square),
                         reads=[rawres], writes=["asq%d" % i])
                    P.op("pe", lambda e, i=i, b=b: e.matmul(ps[b][0:64, :], ones[0:64, :], sq[i][:], start=True, stop=True),
                         reads=["aones", "asq%d" % i], writes=["ps%d" % b])
                    P.op("dve", lambda e, i=i, b=b: e.tensor_scalar(rs[i][:], ps[b][0:64, :], 1.0 / 64, RMS_EPS, ALU.mult, ALU.add),
                         reads=["ps%d" % b], writes=["ars%d" % i])
                    P.op("act", lambda e, i=i: e.activation(rs[i][:], rs[i][:], AF.Sqrt), reads=["ars%d" % i], writes=["ars%d" % i])
                    P.op("dve", lambda e, i=i: e.reciprocal(rs[i][:], rs[i][:]), reads=["ars%d" % i], writes=["ars%d" % i])
                    P.op("dve", lambda e, i=i, raw=raw, dst=dst, g=g, ts_=ts_: e.scalar_tensor_tensor(
                        dst[:, ts_], raw[:, ts_], g[:, 0:1], rs[i][:], ALU.mult, ALU.mult),
                        reads=[rawres, gres, "ars%d" % i], writes=[dres])
            for qb4 in range(NQB // 4):
                bo, bd = self.bank(0, 4), self.bank(0, 4)
                for qq in range(4):
                    qb = qb4 * 4 + qq
                    qs = slice(qb * 128, (qb + 1) * 128)
                    j0 = max(0, 4 - qb)
                    ba, bb = self.bank(4, 8), self.bank(4, 8)
                    for j in range(j0, 5):
                        kt = qb - 4 + j
                        dstp = ps[ba][:, j * 128:(j + 1) * 128] if j < 4 else ps[bb][:, 0:128]
                        P.op("pe", lambda e, dstp=dstp, kt=kt, qs=qs, s=s: e.matmul(
                            dstp, kn[s][:, kt * 128:(kt + 1) * 128], qn[s][:, qs], start=True, stop=True),
                            reads=["akn%d" % s, "aqn%d" % s], writes=["ps%d" % (ba if j < 4 else bb)])
                    pi = self.rot("apt")
                    p_ = pt[pi]
                    if j0 < 4:
                        P.op("dve", lambda e, p_=p_, j0=j0, ba=ba, h=h: e.tensor_tensor(
                            p_[:, j0:4, :], ps[ba][:, j0 * 128:512].rearrange("p (j q) -> p j q", q=128), bias[:, h, j0:4, :], ALU.add),
                            reads=["ps%d" % ba, "abias"], writes=["apt%d" % pi])
                    P.op("dve", lambda e, p_=p_, bb=bb, h=h: e.tensor_tensor(p_[:, 4, :], ps[bb][:, 0:128], bias[:, h, 4, :], ALU.add),
                         reads=["ps%d" % bb, "abias"], writes=["apt%d" % pi])
                    P.op("act", lambda e, p_=p_, j0=j0: e.activation(p_[:, j0:5, :], p_[:, j0:5, :], AF.Exp),
                         reads=["apt%d" % pi], writes=["apt%d" % pi])
                    for j in range(j0, 5):
                        kt = qb - 4 + j
                        P.op("pe", lambda e, p_=p_, j=j, kt=kt, qq=qq, s=s, bo=bo, j0=j0: e.matmul(
                            ps[bo][0:64, qq * 128:(qq + 1) * 128], vt[s][:, kt, :], p_[:, j, :], start=(j == j0), stop=(j == 4)),
                            reads=["avt%d" % s, "apt%d" % pi], writes=["ps%d" % bo])
                    for j in range(j0, 5):
                        P.op("pe", lambda e, p_=p_, j=j, qq=qq, bd=bd, j0=j0: e.matmul(
                            ps[bd][0:64, qq * 128:(qq + 1) * 128], onesr[:, :], p_[:, j, :], start=(j == j0), stop=(j == 4)),
                            reads=["aonesr", "apt%d" % pi], writes=["ps%d" % bd])
                ri = self.rot("arec")
                q4 = slice(qb4 * 512, (qb4 + 1) * 512)
                P.op("dve", lambda e, ri=ri, bd=bd: e.reciprocal(rec[ri][:], ps[bd][0:64, :]), reads=["ps%d" % bd], writes=["arec%d" % ri])
                P.op("dve", lambda e, ri=ri, bo=bo, s=s, q4=q4: e.tensor_tensor(ob[s][:, q4], ps[bo][0:64, :], rec[ri][:], ALU.mult),
                     reads=["ps%d" % bo, "arec%d" % ri], writes=["aob%d" % s])
            P.dma("sp", out_d[:, h, :], ob[s][:], reads=["aob%d" % s])


def att_bias_layout(rel_bias8):
    j = np.arange(5)[:, None, None]
    k = np.arange(128)[None, :, None]
    q = np.arange(128)[None, None, :]
    dist = (4 - j) * 128 + q - k
    ck = (j - 4) * 2 + k // 64
    cq = q // 64
    valid = (ck >= cq - 8) & (ck <= cq)
    idx = np.clip(dist, -63, 128) + 63
    idx = np.where(valid, idx, 192)
    ext = np.concatenate([rel_bias8, np.full((8, 1), NEG, np.float32)], axis=1)
    g = ext[:, idx]
    return np.ascontiguousarray(g.transpose(2, 0, 1, 3))


SEG = 128
NCH = SEG // 64
GN_EPS = 64e-5
LD_C = -math.exp(-0.5)


def rwkv_build(mx, debug=False):
    T, P, ps = mx.T, mx.P, mx.ps
    sb, din = mx.sb, mx.din
    r_d, k_d = din("rw_r", [64, 8, T + 1]), din("rw_k", [64, 8, T + 1])
    v_d, vp_d = din("rw_v", [T, 512]), din("rw_vprev", [T, 512])
    xw_d, xa_d, xg_d = din("rw_xw", [64, T + 1]), din("rw_xa", [64, T + 1]), din("rw_xg", [128, T + 1])
    out_d = mx.dout("rw_out", [T, 512])
    cst = {}

    def const(name, shape):
        d = din(name, shape)
        t = sb("c_" + name, shape)
        P.dma("sp", t[:], d, writes=[name])
        cst[name] = t
        return t

    mu_r, mu_k = const("mu_r", [64, 8]), const("mu_k", [64, 8])
    mu_v = const("mu_v", [64, 512])
    mu_xw, mu_xa, mu_xg = const("mu_xw", [64, 1]), const("mu_xa", [64, 1]), const("mu_xg", [128, 1])
    w_up, a_up, g_up = const("w_up", [64, 8, 64]), const("a_up", [64, 8, 64]), const("g_up", [128, 512])
    w0, a0, k_k, k_a, r_k = const("w0", [64, 8]), const("a0", [64, 8]), const("k_k", [64, 8]), const("k_a", [64, 8]), const("r_k", [64, 8])
    lnw, lnb = const("lnx_w", [64, 512]), const("lnx_b", [64, 512])
    ident, ident8 = const("ident", [64, 64]), const("ident8", [64, 8, 64])
    su, sl, iu = const("m_su", [64, 8, 64]), const("m_sl", [64, 8, 64]), const("m_iu", [64, 8, 64])
    reset = const("m_reset", [64, 8 * SEG])
    ones = sb("rones", [64, 64])
    P.op("pool", lambda e: e.memset(ones[:], 1.0), writes=["rones"])
    H = sb("H", [64, 8, 64])
    P.op("pool", lambda e: e.memset(H[:], 0.0), writes=["H"])

    S3 = [64, 8, SEG]
    zr, zk = sb("zr", [64, 8, SEG + 1]), sb("zk", [64, 8, SEG + 1])
    xw, xa, xg = sb("xw", [64, SEG + 1]), sb("xa", [64, SEG + 1]), sb("xg", [128, SEG + 1])
    xws, xas, xgs = sb("xws", [64, SEG]), sb("xas", [64, SEG]), sb("xgs", [128, SEG])
    xt = sb("xt", [128, SEG])
    r_s, k_s, ld, a_, kk, kp, cum, gc = [sb(n, S3) for n in ("r_s", "k_s", "ld", "a_", "kk", "kp", "cum", "gc")]
    Rt, Kt, Bt, At, t1, t2 = [sb(n, S3) for n in ("Rt", "Kt", "Bt", "At", "t1", "t2")]
    vraw, vprev, vs, g_tm = [sb(n, [64, NCH, 512]) for n in ("vraw", "vprev", "vs", "g_tm")]
    rho = sb("rho", [64, NCH, 8])
    C3 = [64, 8, 64]
    Aa = [sb("Aa%d" % i, C3) for i in range(2)]
    Bb = [sb("Bb%d" % i, C3) for i in range(2)]
    Pm, NakT, MrbT, MrkT, Btm, Ktm, W0, U = [sb(n, C3) for n in ("Pm", "NakT", "MrbT", "MrkT", "Btm", "Ktm", "W0", "U")]
    y, yc, ysq, yo = [sb(n, C3) for n in ("y", "yc", "ysq", "yo")]
    st1, st2 = sb("st1", [64, 8]), sb("st2", [64, 8])

    def bc(t2d):
        return lambda n: t2d.unsqueeze(2).to_broadcast([64, 8, n])

    f2 = lambda t: t[:].rearrange("p h t -> p (h t)")
    vd = v_d.rearrange("(c p) f -> p c f", p=64)
    vpd = vp_d.rearrange("(c p) f -> p c f", p=64)
    od = out_d.rearrange("(c p) f -> p c f", p=64)

    def mm8(bank_, lhs_fn, rhs_fn, reads, n=64, start=True, stop=True):
        for h in range(8):
            P.op("pe", lambda e, h=h: e.matmul(ps[bank_][0:64, h * n:(h + 1) * n], lhs_fn(h), rhs_fn(h), start=start, stop=stop),
                 reads=reads, writes=["ps%d" % bank_])

    ps3 = lambda b: ps[b][0:64, :].rearrange("p (h t) -> p h t", h=8)

    for seg in range(T // SEG):
        t0 = seg * SEG
        P.dma("pool", zr[:], r_d[:, :, t0:t0 + SEG + 1], writes=["zr"])
        P.dma("pool", zk[:], k_d[:, :, t0:t0 + SEG + 1], writes=["zk"])
        P.dma("pool", xw[:], xw_d[:, t0:t0 + SEG + 1], writes=["xw"])
        P.dma("pool", xa[:], xa_d[:, t0:t0 + SEG + 1], writes=["xa"])
        P.dma("pool", xg[:], xg_d[:, t0:t0 + SEG + 1], writes=["xg"])
        P.dma("pool", vraw[:], vd[:, seg * NCH:(seg + 1) * NCH, :], writes=["vraw"])
        P.dma("pool", vprev[:], vpd[:, seg * NCH:(seg + 1) * NCH, :], writes=["vprev"])
        for z, zres, mu, mures, dst, dres in ((zr, "zr", mu_r, "mu_r", r_s, "r_s"), (zk, "zk", mu_k, "mu_k", k_s, "k_s")):
            P.op("dve", lambda e, z=z: e.tensor_tensor(t1[:], z[:, :, 0:SEG], z[:, :, 1:SEG + 1], ALU.subtract), reads=[zres], writes=["t1"])
            P.op("dve", lambda e, mu=mu: e.tensor_tensor(t1[:], t1[:], mu[:].unsqueeze(2).to_broadcast(S3), ALU.mult), reads=["t1", mures], writes=["t1"])
            P.op("dve", lambda e, z=z, dst=dst: e.tensor_tensor(dst[:], t1[:], z[:, :, 1:SEG + 1], ALU.add), reads=["t1", zres], writes=[dres])
        for x, xres, mu, mures, dst, dres, np_ in ((xw, "xw", mu_xw, "mu_xw", xws, "xws", 64), (xa, "xa", mu_xa, "mu_xa", xas, "xas", 64),
                                                  (xg, "xg", mu_xg, "mu_xg", xgs, "xgs", 128)):
            P.op("dve", lambda e, x=x, np_=np_: e.tensor_tensor(xt[0:np_, :], x[:, 0:SEG], x[:, 1:SEG + 1], ALU.subtract), reads=[xres], writes=["xt"])
            P.op("dve", lambda e, x=x, np_=np_, mu=mu, dst=dst: e.scalar_tensor_tensor(dst[:], xt[0:np_, :], mu[:, 0:1], x[:, 1:SEG + 1], ALU.mult, ALU.add),
                 reads=["xt", xres, mures], writes=[dres])
        P.op("pool", lambda e: e.tensor_tensor(vs[:], vprev[:], vraw[:], ALU.subtract), reads=["vprev", "vraw"], writes=["vs"])
        P.op("pool", lambda e: e.tensor_tensor(vs[:], vs[:], mu_v[:].unsqueeze(1).to_broadcast([64, NCH, 512]), ALU.mult), reads=["vs", "mu_v"], writes=["vs"])
        P.op("pool", lambda e: e.tensor_tensor(vs[:], vs[:], vraw[:], ALU.add), reads=["vs", "vraw"], writes=["vs"])
        P.op("act", lambda e: e.activation(xws[:], xws[:], AF.Tanh), reads=["xws"], writes=["xws"])
        P.op("act", lambda e: e.activation(xgs[:], xgs[:], AF.Sigmoid), reads=["xgs"], writes=["xgs"])
        for up, upres, xin, xres, bias, bres, dst, dres in ((w_up, "w_up", xws, "xws", w0, "w0", ld, "ld"), (a_up, "a_up", xas, "xas", a0, "a0", a_, "a_")):
            for hb in range(2):
                b = mx.bank()
                for hh in range(4):
                    h = hb * 4 + hh
                    P.op("pe", lambda e, b=b, hh=hh, h=h, up=up, xin=xin: e.matmul(ps[b][0:64, hh * SEG:(hh + 1) * SEG], up[:, h, :], xin[:], start=True, stop=True),
                         reads=[upres, xres], writes=["ps%d" % b])
                for hh in range(4):
                    h = hb * 4 + hh
                    P.op("act", lambda e, b=b, hh=hh, h=h, dst=dst, bias=bias: e.activation(dst[:, h, :], ps[b][0:64, hh * SEG:(hh + 1) * SEG], AF.Sigmoid, bias=bias[:, h:h + 1]),
                         reads=["ps%d" % b, bres], writes=[dres])
        P.op("dve", lambda e: e.tensor_scalar(ld[:], ld[:], LD_C, None, ALU.mult), reads=["ld"], writes=["ld"])
        for c in range(NCH):
            b = mx.bank()
            P.op("pe", lambda e, b=b, c=c: e.matmul(ps[b][0:64, :], xgs[:, c * 64:(c + 1) * 64], g_up[:], start=True, stop=True),
                 reads=["xgs", "g_up"], writes=["ps%d" % b])
            P.op("act", lambda e, b=b, c=c: e.activation(g_tm[:, c, :], ps[b][0:64, :], AF.Copy), reads=["ps%d" % b], writes=["g_tm"])
        P.op("dve", lambda e: e.tensor_tensor(kk[:], k_s[:], k_k[:].unsqueeze(2).to_broadcast(S3), ALU.mult), reads=["k_s", "k_k"], writes=["kk"])
        P.op("dve", lambda e: e.tensor_tensor(t1[:], kk[:], kk[:], ALU.mult), reads=["kk"], writes=["t1"])
        for hb in range(2):
            b = mx.bank()
            for hh in range(4):
                h = hb * 4 + hh
                P.op("pe", lambda e, b=b, hh=hh, h=h: e.matmul(ps[b][0:64, hh * SEG:(hh + 1) * SEG], ones[:], t1[:, h, :], start=True, stop=True),
                     reads=["rones", "t1"], writes=["ps%d" % b])
            P.op("act", lambda e, b=b, hb=hb: e.activation(t2[:, hb * 4:(hb + 1) * 4, :], ps[b][0:64, :].rearrange("p (h t) -> p h t", h=4), AF.Sqrt),
                 reads=["ps%d" % b], writes=["t2"])
        P.op("dve", lambda e: e.tensor_scalar(t2[:], t2[:], 1e-12, None, ALU.max), reads=["t2"], writes=["t2"])
        P.op("dve", lambda e: e.reciprocal(t2[:], t2[:]), reads=["t2"], writes=["t2"])
        P.op("dve", lambda e: e.tensor_tensor(kk[:], kk[:], t2[:], ALU.mult), reads=["kk", "t2"], writes=["kk"])
        P.op("dve", lambda e: e.tensor_scalar(t1[:], a_[:], -1.0, None, ALU.add), reads=["a_"], writes=["t1"])
        P.op("dve", lambda e: e.tensor_tensor(t1[:], t1[:], k_a[:].unsqueeze(2).to_broadcast(S3), ALU.mult), reads=["t1", "k_a"], writes=["t1"])
        P.op("dve", lambda e: e.scalar_tensor_tensor(kp[:], t1[:], 1.0, k_s[:], ALU.add, ALU.mult), reads=["t1", "k_s"], writes=["kp"])
        P.op("dve", lambda e: e.tensor_tensor(t1[:], r_s[:], kp[:], ALU.mult), reads=["r_s", "kp"], writes=["t1"])
        P.op("dve", lambda e: e.tensor_tensor(t1[:], t1[:], r_k[:].unsqueeze(2).to_broadcast(S3), ALU.mult), reads=["t1", "r_k"], writes=["t1"])
        b = mx.bank()
        for c in range(NCH):
            for h in range(8):
                o = (c * 8 + h) * 2
                P.op("pe", lambda e, b=b, c=c, h=h, o=o: e.matmul(ps[b][0:64, o:o + 2], t1[:, h, c * 64:(c + 1) * 64], ones[:, 0:2], start=True, stop=True),
                     reads=["t1", "rones"], writes=["ps%d" % b])
        P.op("act", lambda e, b=b: e.activation(rho[:], ps[b][0:64, 0:NCH * 16].rearrange("p (c h two) -> p c h two", c=NCH, two=2)[:, :, :, 0], AF.Copy),
             reads=["ps%d" % b], writes=["rho"])
        P.op("dve", lambda e: e.tensor_tensor_scan(f2(cum), reset[:], f2(ld), 0.0, ALU.mult, ALU.add), reads=["m_reset", "ld"], writes=["cum"])
        P.op("dve", lambda e: e.tensor_tensor(t2[:], cum[:], ld[:], ALU.subtract), reads=["cum", "ld"], writes=["t2"])
        P.op("act", lambda e: e.activation(t2[:], t2[:], AF.Exp), reads=["t2"], writes=["t2"])
        P.op("dve", lambda e: e.scalar_tensor_tensor(At[:], kk[:], -1.0, t2[:], ALU.mult, ALU.mult), reads=["kk", "t2"], writes=["At"])
        P.op("act", lambda e: e.activation(gc[:], cum[:], AF.Exp), reads=["cum"], writes=["gc"])
        P.op("dve", lambda e: e.tensor_tensor(Rt[:], r_s[:], gc[:], ALU.mult), reads=["r_s", "gc"], writes=["Rt"])
        P.op("act", lambda e: e.activation(t1[:], cum[:], AF.Exp, scale=-1.0), reads=["cum"], writes=["t1"])
        P.op("dve", lambda e: e.tensor_tensor(Kt[:], kp[:], t1[:], ALU.mult), reads=["kp", "t1"], writes=["Kt"])
        P.op("dve", lambda e: e.tensor_tensor(Bt[:], kk[:], a_[:], ALU.mult), reads=["kk", "a_"], writes=["Bt"])
        P.op("dve", lambda e: e.tensor_tensor(Bt[:], Bt[:], t1[:], ALU.mult), reads=["Bt", "t1"], writes=["Bt"])

        for c in range(NCH):
            cs = slice(c * 64, (c + 1) * 64)
            vh = lambda h, c=c: vs[:, c, h * 64:(h + 1) * 64]
            for lhs, lres, rhs, rres, mask, mres, dst, dres in ((Bt, "Bt", At, "At", su, "m_su", Aa[0], "Aa0"), (At, "At", Bt, "Bt", sl, "m_sl", Bb[0], "Bb0"),
                                                                (Kt, "Kt", At, "At", su, "m_su", NakT, "NakT"), (Bt, "Bt", Rt, "Rt", iu, "m_iu", MrbT, "MrbT"),
                                                                (Kt, "Kt", Rt, "Rt", iu, "m_iu", MrkT, "MrkT")):
                b = mx.bank()
                mm8(b, lambda h, lhs=lhs: lhs[:, h, cs], lambda h, rhs=rhs: rhs[:, h, cs], [lres, rres])
                P.op("dve", lambda e, b=b, dst=dst, mask=mask: e.tensor_tensor(dst[:], ps3(b), mask[:], ALU.mult), reads=["ps%d" % b, mres], writes=[dres])
            for src, sres, dst, dres in ((Bt, "Bt", Btm, "Btm"), (Kt, "Kt", Ktm, "Ktm")):
                b = mx.bank()
                for h in range(8):
                    P.op("pe", lambda e, b=b, h=h, src=src: e.transpose(ps[b][0:64, h * 64:(h + 1) * 64], src[:, h, cs], ident[:]),
                         reads=[sres, "ident"], writes=["ps%d" % b])
                P.op("act", lambda e, b=b, dst=dst: e.activation(dst[:], ps3(b), AF.Copy), reads=["ps%d" % b], writes=[dres])
            P.op("dve", lambda e: e.tensor_tensor(Pm[:], ident8[:], Aa[0][:], ALU.add), reads=["ident8", "Aa0"], writes=["Pm"])
            cur = 0
            for lvl in range(1, 6):
                nxt = 1 - cur
                if lvl < 5:
                    b = mx.bank()
                    mm8(b, lambda h, cur=cur: Bb[cur][:, h, :], lambda h, cur=cur: Aa[cur][:, h, :], ["Aa%d" % cur, "Bb%d" % cur])
                    P.op("act", lambda e, b=b, nxt=nxt: e.activation(Aa[nxt][:], ps3(b), AF.Copy), reads=["ps%d" % b], writes=["Aa%d" % nxt])
                b = mx.bank()
                mm8(b, lambda h, cur=cur: Aa[cur][:, h, :], lambda h, cur=cur: Bb[cur][:, h, :], ["Aa%d" % cur, "Bb%d" % cur])
                P.op("act", lambda e, b=b, nxt=nxt: e.activation(Bb[nxt][:], ps3(b), AF.Copy), reads=["ps%d" % b], writes=["Bb%d" % nxt])
                b = mx.bank()
                mm8(b, lambda h, nxt=nxt: Bb[nxt][:, h, :], lambda h: Pm[:, h, :], ["Bb%d" % nxt, "Pm"])
                P.op("dve", lambda e, b=b: e.tensor_tensor(Pm[:], ps3(b), Pm[:], ALU.add), reads=["ps%d" % b, "Pm"], writes=["Pm"])
                cur = nxt
            b = mx.bank()
            for h in range(8):
                P.op("pe", lambda e, b=b, h=h: e.matmul(ps[b][0:64, h * 64:(h + 1) * 64], NakT[:, h, :], vh(h), start=True, stop=False),
                     reads=["NakT", "vs"], writes=["ps%d" % b])
                P.op("pe", lambda e, b=b, h=h: e.matmul(ps[b][0:64, h * 64:(h + 1) * 64], At[:, h, cs], H[:, h, :], start=False, stop=True),
                     reads=["At", "H"], writes=["ps%d" % b])
            P.op("act", lambda e, b=b: e.activation(W0[:], ps3(b), AF.Copy), reads=["ps%d" % b], writes=["W0"])
            b = mx.bank()
            mm8(b, lambda h: Pm[:, h, :], lambda h: W0[:, h, :], ["Pm", "W0"])
            P.op("act", lambda e, b=b: e.activation(U[:], ps3(b), AF.Copy), reads=["ps%d" % b], writes=["U"])
            by = mx.bank()
            for h in range(8):
                o = ps[by][0:64, h * 64:(h + 1) * 64]
                P.op("pe", lambda e, o=o, h=h: e.matmul(o, Rt[:, h, cs], H[:, h, :], start=True, stop=False), reads=["Rt", "H"], writes=["ps%d" % by])
                P.op("pe", lambda e, o=o, h=h: e.matmul(o, MrbT[:, h, :], U[:, h, :], start=False, stop=False), reads=["MrbT", "U"], writes=["ps%d" % by])
                P.op("pe", lambda e, o=o, h=h: e.matmul(o, MrkT[:, h, :], vh(h), start=False, stop=True), reads=["MrkT", "vs"], writes=["ps%d" % by])
            bh = mx.bank()
            for h in range(8):
                o = ps[bh][0:64, h * 64:(h + 1) * 64]
                P.op("pe", lambda e, o=o, h=h: e.matmul(o, Btm[:, h, :], U[:, h, :], start=True, stop=False), reads=["Btm", "U"], writes=["ps%d" % bh])
                P.op("pe", lambda e, o=o, h=h: e.matmul(o, Ktm[:, h, :], vh(h), start=False, stop=False), reads=["Ktm", "vs"], writes=["ps%d" % bh])
                P.op("pe", lambda e, o=o, h=h: e.matmul(o, ident[:], H[:, h, :], start=False, stop=True), reads=["ident", "H"], writes=["ps%d" % bh])
            P.op("dve", lambda e, bh=bh, c=c: e.tensor_tensor(H[:], ps3(bh), gc[:, :, c * 64 + 63].unsqueeze(2).to_broadcast(C3), ALU.mult),
                 reads=["ps%d" % bh, "gc"], writes=["H"])
            P.op("act", lambda e, by=by: e.activation(y[:], ps3(by), AF.Copy), reads=["ps%d" % by], writes=["y"])
            P.op("dve", lambda e: e.tensor_reduce(st1[:], y[:], AX.X, ALU.add), reads=["y"], writes=["st1"])
            P.op("dve", lambda e: e.tensor_scalar(st1[:], st1[:], -1.0 / 64, None, ALU.mult), reads=["st1"], writes=["st1"])
            P.op("pool", lambda e: e.tensor_tensor(yc[:], y[:], st1[:].unsqueeze(2).to_broadcast(C3), ALU.add), reads=["y", "st1"], writes=["yc"])
            P.op("pool", lambda e: e.tensor_tensor(ysq[:], yc[:], yc[:], ALU.mult), reads=["yc"], writes=["ysq"])
            P.op("dve", lambda e: e.tensor_reduce(st2[:], ysq[:], AX.X, ALU.add), reads=["ysq"], writes=["st2"])
            P.op("dve", lambda e: e.tensor_scalar(st2[:], st2[:], 1.0 / 64, GN_EPS, ALU.mult, ALU.add), reads=["st2"], writes=["st2"])
            P.op("act", lambda e: e.activation(st2[:], st2[:], AF.Sqrt), reads=["st2"], writes=["st2"])
            P.op("dve", lambda e: e.reciprocal(st2[:], st2[:]), reads=["st2"], writes=["st2"])
            y2 = lambda t: t[:].rearrange("p h v -> p (h v)")
            P.op("pool", lambda e: e.tensor_tensor(yc[:], yc[:], st2[:].unsqueeze(2).to_broadcast(C3), ALU.mult), reads=["yc", "st2"], writes=["yc"])
            P.op("pool", lambda e: e.tensor_tensor(y2(yc), y2(yc), lnw[:], ALU.mult), reads=["yc", "lnx_w"], writes=["yc"])
            P.op("pool", lambda e: e.tensor_tensor(y2(yc), y2(yc), lnb[:], ALU.add), reads=["yc", "lnx_b"], writes=["yc"])
            P.op("pool", lambda e, c=c: e.tensor_tensor(ysq[:], vs[:, c, :].rearrange("p (h v) -> p h v", h=8), rho[:, c, :].unsqueeze(2).to_broadcast(C3), ALU.mult),
                 reads=["vs", "rho"], writes=["ysq"])
            P.op("pool", lambda e: e.tensor_tensor(yc[:], yc[:], ysq[:], ALU.add), reads=["yc", "ysq"], writes=["yc"])
            P.op("pool", lambda e, c=c: e.tensor_tensor(y2(yo), y2(yc), g_tm[:, c, :], ALU.mult), reads=["yc", "g_tm"], writes=["yo"])
            P.dma("sp", od[:, seg * NCH + c, :], y2(yo), reads=["yo"])
    if debug:
        for nm, t in (("r_s", r_s), ("k_s", k_s), ("ld", ld), ("a_", a_), ("kk", kk), ("kp", kp), ("gc", gc), ("Rt", Rt), ("Kt", Kt), ("Bt", Bt), ("At", At),
                      ("vs", vs), ("g_tm", g_tm), ("rho", rho), ("Aa0", Aa[0]), ("Pm", Pm), ("NakT", NakT), ("MrbT", MrbT), ("MrkT", MrkT), ("Btm", Btm),
                      ("W0", W0), ("U", U), ("y", y), ("yc", yc), ("H", H), ("xws", xws), ("xgs", xgs), ("cum", cum)):
            dd_ = mx.dout("dbg_" + nm, list(t[:].shape))
            P.dma("sp", dd_, t[:], reads=[nm])


def rwkv_host_inputs(z, hh, mu, w0, w_up, a0, a_up, g_up, k_k, k_a, r_k, lnx_w, lnx_b):
    T = z.shape[0]
    cs = slice(hh * 512, (hh + 1) * 512)

    def fm_pad(a):
        o = np.zeros((64, 8, T + 1), np.float32)
        o[:, :, 1:] = a.reshape(T, 8, 64).transpose(2, 1, 0)
        return o

    def fm_pad2(a):
        o = np.zeros((a.shape[1], T + 1), np.float32)
        o[:, 1:] = a.T
        return o

    h64 = lambda a: np.ascontiguousarray(a[cs].reshape(8, 64).T)
    rep = lambda a: np.ascontiguousarray(np.broadcast_to(a[cs][None, :], (64, 512)))
    v = z[:, 2048:3072][:, cs]
    vprev = np.concatenate([np.zeros((1, 512), np.float32), v[:-1]], 0)
    tri = np.triu(np.ones((64, 64), np.float32), 1)
    rep8 = lambda m: np.ascontiguousarray(np.broadcast_to(m[:, None, :], (64, 8, 64)))
    reset = np.ones((64, 8 * SEG), np.float32)
    reset[:, ::64] = 0.0
    return {
        "rw_r": fm_pad(z[:, 0:1024][:, cs]), "rw_k": fm_pad(z[:, 1024:2048][:, cs]),
        "rw_v": np.ascontiguousarray(v), "rw_vprev": vprev,
        "rw_xw": fm_pad2(z[:, 3072:3136]), "rw_xa": fm_pad2(z[:, 3136:3200]), "rw_xg": fm_pad2(z[:, 3200:3328]),
        "mu_r": h64(mu[0:1024]), "mu_k": h64(mu[1024:2048]), "mu_v": rep(mu[2048:3072]),
        "mu_xw": mu[3072:3136, None].copy(), "mu_xa": mu[3136:3200, None].copy(), "mu_xg": mu[3200:3328, None].copy(),
        "w_up": np.ascontiguousarray(w_up[:, cs].reshape(64, 8, 64)), "a_up": np.ascontiguousarray(a_up[:, cs].reshape(64, 8, 64)),
        "g_up": np.ascontiguousarray(g_up[:, cs]),
        "w0": h64(w0), "a0": h64(a0), "k_k": h64(k_k), "k_a": h64(k_a), "r_k": h64(r_k.reshape(-1)),
        "lnx_w": rep(lnx_w), "lnx_b": rep(lnx_b),
        "ident": np.eye(64, dtype=np.float32), "ident8": rep8(np.eye(64, dtype=np.float32)),
        "m_su": rep8(tri), "m_sl": rep8(tri.T.copy()), "m_iu": rep8(np.triu(np.ones((64, 64), np.float32), 0)),
        "m_reset": reset,
    }


NCORES = 8
BATCH, SEQ = 4, 4096
HALF = SEQ // 2
_CACHE = {}


def _prog(key, builder):
    if key not in _CACHE:
        _CACHE[key] = builder()
    return _CACHE[key]


def _run(nc, in_maps):
    res = run_bass_kernel_spmd(nc, in_maps, core_ids=list(range(NCORES)))
    return res.results


def _ffn_maps(pfx, wg, wu, wd, gn):
    return {pfx + "_g": wlayout(wg), pfx + "_u": wlayout(wu), pfx + "_d": wlayout(wd), pfx + "_n": glayout(gn)}


def kernel(x, p, ffn1_norm, ffn1_w_gate, ffn1_w_up, ffn1_w_down, mix_norm,
           ffn2_norm, ffn2_w_gate, ffn2_w_up, ffn2_w_down,
           ple_norm, ple_w_gate, ple_w_proj,
           ab_w_in, att_q_gain, att_k_gain, att_rel_bias, rwkv_mu, rwkv_w0,
           rwkv_w_up, rwkv_a0, rwkv_a_up, rwkv_g_up, rwkv_k_k, rwkv_k_a,
           rwkv_r_k, rwkv_lnx_w, rwkv_lnx_b, ab_w_out,
           ssm_w_in, ssm_lambda_re, ssm_lambda_im, ssm_log_dt, ssm_b_re, ssm_b_im,
           ssm_c_re, ssm_c_im, ssm_d, ssm_w_out):
    f = lambda a: np.asarray(a, dtype=np.float32)
    x, p = f(x), f(p)
    cores = [(c // 2, c % 2) for c in range(NCORES)]
    tok = lambda hf: slice(hf * HALF, (hf + 1) * HALF)

    shared = {}
    shared.update(_ffn_maps("f1", f(ffn1_w_gate[0]), f(ffn1_w_up[0]), f(ffn1_w_down[0]), f(ffn1_norm[0])))
    shared["mix_n"] = glayout(f(mix_norm[0]))
    shared["win"] = wlayout(f(ab_w_in[0]))
    ims = [dict(shared, h_in=fm(x[b, tok(hf)])) for b, hf in cores]
    rA = _run(_prog("A", lambda: build_dense("A", HALF)), ims)
    del shared, ims
    h1 = [r["h_out"] for r in rA]
    z = [np.concatenate([rA[2 * b]["proj"], rA[2 * b + 1]["proj"]], axis=1).T for b in range(BATCH)]

    ims = []
    for b, hh in cores:
        hs = slice(hh * 512, (hh + 1) * 512)
        q = z[b][:, 0:1024][:, hs].reshape(SEQ, 8, 64)
        k = z[b][:, 1024:2048][:, hs].reshape(SEQ, 8, 64)
        v = z[b][:, 2048:3072][:, hs].reshape(SEQ, 8, 64)
        ims.append({"att_q": np.ascontiguousarray(q.transpose(2, 1, 0)), "att_k": np.ascontiguousarray(k.transpose(2, 1, 0)),
                    "att_v": np.ascontiguousarray(v), "att_bias": att_bias_layout(f(att_rel_bias[0])[hh * 8:(hh + 1) * 8]),
                    "att_qg": f(att_q_gain[0])[:, None].copy(), "att_kg": f(att_k_gain[0])[:, None].copy()})

    def build_att():
        mx = Mixer(SEQ)
        mx.attention()
        return mx.finish()

    rB1 = _run(_prog("B1", build_att), ims)

    ims = [rwkv_host_inputs(z[b][:, 3072:], hh, f(rwkv_mu[0]), f(rwkv_w0[0]), f(rwkv_w_up[0]), f(rwkv_a0[0]), f(rwkv_a_up[0]),
                            f(rwkv_g_up[0]), f(rwkv_k_k[0]), f(rwkv_k_a[0]), f(rwkv_r_k[0]), f(rwkv_lnx_w[0]), f(rwkv_lnx_b[0]))
           for b, hh in cores]

    def build_rw():
        mx = Mixer(SEQ)
        rwkv_build(mx)
        return mx.finish()

    rB2 = _run(_prog("B2", build_rw), ims)
    del z
    mix = []
    for b in range(BATCH):
        parts = [rB1[2 * b + hh]["att_out"].transpose(1, 0, 2).reshape(512, SEQ) for hh in range(2)]
        parts += [rB2[2 * b + hh]["rw_out"].T for hh in range(2)]
        mix.append(np.concatenate(parts, axis=0))

    shared = {"wout": wlayout(f(ab_w_out[0])), "ple_n": glayout(f(ple_norm[0])), "pg": wlayout(f(ple_w_gate[0])), "pp": wlayout(f(ple_w_proj[0])),
              "mix_n": glayout(f(mix_norm[1])), "win": wlayout(f(ssm_w_in[0]))}
    shared.update(_ffn_maps("f2", f(ffn2_w_gate[0]), f(ffn2_w_up[0]), f(ffn2_w_down[0]), f(ffn2_norm[0])))
    shared.update(_ffn_maps("f1", f(ffn1_w_gate[1]), f(ffn1_w_up[1]), f(ffn1_w_down[1]), f(ffn1_norm[1])))
    ims = [dict(shared, h_in=h1[c], mix=np.ascontiguousarray(mix[b][:, tok(hf)]), p=fm(p[0, b, tok(hf)])) for c, (b, hf) in enumerate(cores)]
    rC = _run(_prog("C", lambda: build_dense("C", HALF)), ims)
    del shared, ims, mix
    h2 = [r["h_out"] for r in rC]
    u = [np.concatenate([rC[2 * b]["proj"], rC[2 * b + 1]["proj"]], axis=1) for b in range(BATCH)]

    ims = [s5_host_inputs(u[b][gh * 512:(gh + 1) * 512], f(ssm_lambda_re[0]), f(ssm_lambda_im[0]), f(ssm_log_dt[0]), f(ssm_b_re[0]), f(ssm_b_im[0]),
                          f(ssm_c_re[0]), f(ssm_c_im[0]), f(ssm_d[0]), gh) for b, gh in cores]
    rD = _run(_prog("D", lambda: build_s5(SEQ)), ims)
    ymix = [np.concatenate([rD[2 * b]["y_out"], rD[2 * b + 1]["y_out"]], axis=0) for b in range(BATCH)]

    shared = {"wout": wlayout(f(ssm_w_out[0])), "ple_n": glayout(f(ple_norm[1])), "pg": wlayout(f(ple_w_gate[1])), "pp": wlayout(f(ple_w_proj[1]))}
    shared.update(_ffn_maps("f2", f(ffn2_w_gate[1]), f(ffn2_w_up[1]), f(ffn2_w_down[1]), f(ffn2_norm[1])))
    ims = [dict(shared, h_in=h2[c], mix=np.ascontiguousarray(ymix[b][:, tok(hf)]), p=fm(p[1, b, tok(hf)])) for c, (b, hf) in enumerate(cores)]
    rE = _run(_prog("E", lambda: build_dense("E", HALF)), ims)
    out = np.empty((BATCH, SEQ, D), np.float32)
    for c, (b, hf) in enumerate(cores):
        out[b, tok(hf)] = rE[c]["h_out"].T
    return out
```
